# Optimizing a Trainium2 kernel written in Bass

```python
import math
import jax, jax.numpy as jnp
from jax import lax
import numpy as np

D_MODEL = 1024
BATCH = 4
SEQ = 4096
DEPTH = 2

CHUNK = 64
N_MIXERS = 2
N_RWKV = (DEPTH + 1) // 2
N_SSM = DEPTH // 2
RWKV_HEAD = 64
RWKV_HEADS = D_MODEL // RWKV_HEAD
DECAY_LORA = 64
AAA_LORA = 64
GATE_LORA = 160
GN_EPS = 64e-5
SSM_GROUP = 16
SSM_GROUPS = D_MODEL // SSM_GROUP
SSM_STATE = 64
D_FF = 4 * D_MODEL
DN_ALPHA = (2.0 * DEPTH) ** 0.25
DN_BETA = (8.0 * DEPTH) ** -0.25
LN_EPS = 1e-5

kernel_name = "rwkv7_s5_interleaved_deepnorm_trunk"


def layer_norm(x, g, b):
    xf = x.astype(jnp.float32)
    mu = jnp.mean(xf, axis=-1, keepdims=True)
    var = jnp.mean(jnp.square(xf - mu), axis=-1, keepdims=True)
    return ((xf - mu) * lax.rsqrt(var + LN_EPS) * g + b).astype(x.dtype)


def cmul(ar, ai, br, bi):
    return ar * br - ai * bi, ar * bi + ai * br


def rwkv7_time_mix(x, mu, w0, w1, w2, a0, a1, a2, g1, g2, k_k, k_a, r_k,
                   wr, wk, wv, wo, lnx_g, lnx_b):
    bsz, seq, d = x.shape
    f32 = jnp.float32
    xx = jnp.pad(x, ((0, 0), (1, 0), (0, 0)))[:, :-1] - x
    xr = x + xx * mu[0]
    xw = x + xx * mu[1]
    xk = x + xx * mu[2]
    xv = x + xx * mu[3]
    xa = x + xx * mu[4]
    xg = x + xx * mu[5]
    r = xr @ wr
    w_pre = (w0 + jnp.tanh(xw @ w1) @ w2).astype(f32)
    decay = jnp.exp(-jnp.exp(-jax.nn.softplus(-w_pre) - 0.5))
    k = xk @ wk
    v = xv @ wv
    a = jax.nn.sigmoid(a0 + (xa @ a1) @ a2)
    g = jax.nn.sigmoid(xg @ g1) @ g2

    def heads(t):
        return t.astype(f32).reshape(bsz, seq, RWKV_HEADS, RWKV_HEAD)

    kk = heads(k * k_k)
    kk = kk / jnp.maximum(jnp.sqrt(jnp.sum(kk * kk, axis=-1, keepdims=True)), 1e-12)
    k = k * (1.0 + (a - 1.0) * k_a)
    rh, kh, vh, ah, wh = heads(r), heads(k), heads(v), heads(a), heads(decay)
    seq_in = tuple(jnp.moveaxis(t, 1, 0) for t in (rh, wh, kh, vh, -kk, kk * ah))

    def step(state, inp):
        r_t, w_t, k_t, v_t, a_t, b_t = inp
        sa = jnp.einsum('bhvk,bhk->bhv', state, a_t)
        state = (state * w_t[:, :, None, :] + sa[..., None] * b_t[:, :, None, :]
                 + v_t[..., None] * k_t[:, :, None, :])
        return state, jnp.einsum('bhvk,bhk->bhv', state, r_t)

    s0 = jnp.zeros((bsz, RWKV_HEADS, RWKV_HEAD, RWKV_HEAD), f32)
    _, o = lax.scan(step, s0, seq_in)
    o = jnp.moveaxis(o, 0, 1)
    om = jnp.mean(o, axis=-1, keepdims=True)
    ov = jnp.mean(jnp.square(o - om), axis=-1, keepdims=True)
    o = ((o - om) * lax.rsqrt(ov + GN_EPS)).reshape(bsz, seq, d) * lnx_g + lnx_b
    bonus = jnp.sum(rh * kh * r_k.astype(f32), axis=-1, keepdims=True) * vh
    o = o + bonus.reshape(bsz, seq, d)
    return (o.astype(x.dtype) * g) @ wo


def s5_mix(x, a_re, a_im, log_dt, b_re, b_im, c_re, c_im, d_skip, w_glu):
    bsz, seq, d = x.shape
    f32 = jnp.float32
    xf = x.astype(f32)
    dt = jnp.exp(log_dt.astype(f32))[:, None]
    lam_re = jnp.minimum(a_re.astype(f32), -1e-4)
    lam_im = a_im.astype(f32)
    mag = jnp.exp(dt * lam_re)
    abar_re = mag * jnp.cos(dt * lam_im)
    abar_im = mag * jnp.sin(dt * lam_im)
    den = lam_re * lam_re + lam_im * lam_im
    nr, ni = abar_re - 1.0, abar_im
    coef_re = (nr * lam_re + ni * lam_im) / den
    coef_im = (ni * lam_re - nr * lam_im) / den
    bbar_re, bbar_im = cmul(coef_re[..., None], coef_im[..., None],
                            b_re.astype(f32), b_im.astype(f32))
    cr_w, ci_w = c_re.astype(f32), c_im.astype(f32)

    n_chunks = seq // CHUNK
    u = xf.reshape(bsz, n_chunks, CHUNK, SSM_GROUPS, SSM_GROUP).transpose(1, 2, 0, 3, 4)
    a_ch_re = jnp.broadcast_to(abar_re, (CHUNK, 1, SSM_GROUPS, SSM_STATE))
    a_ch_im = jnp.broadcast_to(abar_im, (CHUNK, 1, SSM_GROUPS, SSM_STATE))

    def combine(e1, e2):
        a1r, a1i, b1r, b1i = e1
        a2r, a2i, b2r, b2i = e2
        ar, ai = cmul(a2r, a2i, a1r, a1i)
        br, bi = cmul(a2r, a2i, b1r, b1i)
        return ar, ai, br + b2r, bi + b2i

    def chunk_step(h, u_c):
        h_re, h_im = h
        bu_re = jnp.einsum('tbgc,gpc->tbgp', u_c, bbar_re)
        bu_im = jnp.einsum('tbgc,gpc->tbgp', u_c, bbar_im)
        pr, pi, sr, si = lax.associative_scan(combine, (a_ch_re, a_ch_im, bu_re, bu_im), axis=0)
        carry_re, carry_im = cmul(pr, pi, h_re[None], h_im[None])
        s_re = sr + carry_re
        s_im = si + carry_im
        y = (jnp.einsum('tbgp,gcp->tbgc', s_re, cr_w)
             - jnp.einsum('tbgp,gcp->tbgc', s_im, ci_w))
        return (s_re[-1], s_im[-1]), y

    h0 = jnp.zeros((bsz, SSM_GROUPS, SSM_STATE), f32)
    _, y = lax.scan(chunk_step, (h0, h0), u)
    y = y.transpose(2, 0, 1, 3, 4).reshape(bsz, seq, d) + xf * d_skip.astype(f32)
    y = jax.nn.gelu(y).astype(x.dtype)
    z = y @ w_glu
    return z[..., :d] * jax.nn.sigmoid(z[..., d:])


def sq_relu_mlp(x, w1, w2):
    return jnp.square(jax.nn.relu(x @ w1)) @ w2


def setup_inputs(seed: int = 0) -> dict:
    key = jax.random.key(seed)
    ks = list(jax.random.split(key, 40))
    f32 = jnp.float32

    def nrm(i, shape, scale):
        return scale * jax.random.normal(ks[i], shape, f32)

    def uni(i, shape, lo, hi):
        return jax.random.uniform(ks[i], shape, f32, lo, hi)

    C, H, N = D_MODEL, RWKV_HEADS, RWKV_HEAD
    G, P, S = SSM_GROUPS, SSM_STATE, SSM_GROUP
    nr_, ns_ = N_RWKV, N_SSM
    glu = nrm(30, (ns_, C, 2 * C), C ** -0.5)
    glu = glu * jnp.concatenate([jnp.full((C,), DN_BETA, f32), jnp.ones((C,), f32)])
    a_im = jnp.pi * jnp.arange(P, dtype=f32)[None, None, :] + nrm(22, (ns_, G, P), 0.01)
    return {
        "x": nrm(0, (BATCH, SEQ, C), 1.0),
        "ln_g": 1.0 + nrm(1, (2 * DEPTH, C), 0.02),
        "ln_b": nrm(2, (2 * DEPTH, C), 0.02),
        "rw_mu": uni(3, (nr_, 6, C), 0.0, 1.0),
        "rw_w0": uni(4, (nr_, C), -6.0, -1.0),
        "rw_w1": nrm(5, (nr_, C, DECAY_LORA), C ** -0.5),
        "rw_w2": nrm(6, (nr_, DECAY_LORA, C), 0.1 * DECAY_LORA ** -0.5),
        "rw_a0": nrm(7, (nr_, C), 0.1),
        "rw_a1": nrm(8, (nr_, C, AAA_LORA), C ** -0.5),
        "rw_a2": nrm(9, (nr_, AAA_LORA, C), 0.1 * AAA_LORA ** -0.5),
        "rw_g1": nrm(10, (nr_, C, GATE_LORA), C ** -0.5),
        "rw_g2": nrm(11, (nr_, GATE_LORA, C), GATE_LORA ** -0.5),
        "rw_k_k": 0.85 + nrm(12, (nr_, C), 0.02),
        "rw_k_a": 1.0 + nrm(13, (nr_, C), 0.02),
        "rw_r_k": -0.04 + nrm(14, (nr_, H, N), 0.02),
        "rw_wr": nrm(15, (nr_, C, C), C ** -0.5),
        "rw_wk": nrm(16, (nr_, C, C), C ** -0.5),
        "rw_wv": nrm(17, (nr_, C, C), C ** -0.5),
        "rw_wo": nrm(18, (nr_, C, C), DN_BETA * C ** -0.5),
        "rw_lnx_g": 1.0 + nrm(19, (nr_, C), 0.02),
        "rw_lnx_b": nrm(20, (nr_, C), 0.02),
        "s5_a_re": -0.5 + nrm(21, (ns_, G, P), 0.01),
        "s5_a_im": a_im,
        "s5_log_dt": uni(23, (ns_, G), math.log(1e-3), math.log(1e-1)),
        "s5_b_re": nrm(24, (ns_, G, P, S), (2.0 * S) ** -0.5),
        "s5_b_im": nrm(25, (ns_, G, P, S), (2.0 * S) ** -0.5),
        "s5_c_re": nrm(26, (ns_, G, S, P), (2.0 * P) ** -0.5),
        "s5_c_im": nrm(27, (ns_, G, S, P), (2.0 * P) ** -0.5),
        "s5_d": nrm(28, (ns_, C), 1.0),
        "s5_w_glu": glu,
        "mlp_w1": nrm(31, (DEPTH, C, D_FF), C ** -0.5),
        "mlp_w2": nrm(32, (DEPTH, D_FF, C), DN_BETA * D_FF ** -0.5),
    }


def reference(x, ln_g, ln_b, rw_mu, rw_w0, rw_w1, rw_w2, rw_a0, rw_a1, rw_a2, rw_g1, rw_g2,
              rw_k_k, rw_k_a, rw_r_k, rw_wr, rw_wk, rw_wv, rw_wo, rw_lnx_g, rw_lnx_b,
              s5_a_re, s5_a_im, s5_log_dt, s5_b_re, s5_b_im, s5_c_re, s5_c_im, s5_d, s5_w_glu,
              mlp_w1, mlp_w2):
    h = x
    for i in range(DEPTH):
        j = i // N_MIXERS
        if i % N_MIXERS == 0:
            mix = rwkv7_time_mix(h, rw_mu[j], rw_w0[j], rw_w1[j], rw_w2[j], rw_a0[j], rw_a1[j],
                                 rw_a2[j], rw_g1[j], rw_g2[j], rw_k_k[j], rw_k_a[j], rw_r_k[j],
                                 rw_wr[j], rw_wk[j], rw_wv[j], rw_wo[j], rw_lnx_g[j], rw_lnx_b[j])
        else:
            mix = s5_mix(h, s5_a_re[j], s5_a_im[j], s5_log_dt[j], s5_b_re[j], s5_b_im[j],
                         s5_c_re[j], s5_c_im[j], s5_d[j], s5_w_glu[j])
        h = layer_norm(DN_ALPHA * h + mix, ln_g[2 * i], ln_b[2 * i])
        h = layer_norm(DN_ALPHA * h + sq_relu_mlp(h, mlp_w1[i], mlp_w2[i]),
                       ln_g[2 * i + 1], ln_b[2 * i + 1])
    return h
```

```python
import contextlib
import numpy as np
import concourse.bass as bass
import concourse.mybir as mybir
from concourse.bass_utils import run_bass_kernel_spmd

F32 = mybir.dt.float32
BF16 = mybir.dt.bfloat16
ALU = mybir.AluOpType
AF = mybir.ActivationFunctionType
AX = mybir.AxisListType

C = 1024
DFF = 4096
DEPTH = 2
DN_ALPHA = (2.0 * DEPTH) ** 0.25
LN_EPS = 1e-5
ENGS = ["pe", "dve", "act", "pool", "sp"]


class Sched:
    def __init__(self, nc, es):
        self.nc = nc
        self.es = es
        self.es_sem = es
        self.ops = {e: [] for e in ENGS}
        self.streams = {}
        self.known = {e: {} for e in ENGS}
        self.res = {}
        self.nops = 0
        self.last = {}
        self.free16 = []
        self.uid = 0
        for e in ENGS:
            self._mkstream(e, 1)

    def _mkstream(self, name, inc):
        if inc == 16 and self.free16:
            self.streams[name] = self.free16.pop()
            return
        sem = self.es_sem.enter_context(self.nc.semaphore("s_" + name))
        self.uid += 1
        self.streams[name] = [sem, inc, 0, self.uid]

    def recycle(self):
        for name in list(self.streams):
            if name.startswith("d_"):
                self.free16.append(self.streams.pop(name))
        self.res.clear()

    def sbuf(self, name, shape, dt):
        return self.es.enter_context(self.nc.sbuf_tensor(name, list(shape), dt))

    def psum(self, name, shape, dt):
        return self.es.enter_context(self.nc.psum_tensor(name, list(shape), dt))

    def force_signal_last(self, e):
        rec = self.last.get(e)
        if rec is not None and not rec["signal"]:
            rec["signal"] = True
            self.streams[e][2] += 1

    def op(self, e, fn, reads=(), writes=(), signal=True, dma=None, after_self=False):
        waits = {}
        if after_self:
            waits[e] = self.streams[e][2] * self.streams[e][1]

        def need(sv, kind):
            s, v = sv
            if s == e and kind != "raw":
                return
            if self.streams[s][1] == 16:
                v = self.streams[s][2] * 16
            if waits.get(s, 0) < v:
                waits[s] = v

        for r in reads:
            ent = self.res.get(r)
            if ent and ent[0]:
                need(ent[0], "raw")
        for w in writes:
            ent = self.res.get(w)
            if ent:
                if ent[0]:
                    need(ent[0], "waw")
                for s, v in ent[1].items():
                    need((s, v), "war")
        wl = []
        for s, v in waits.items():
            u = self.streams[s][3]
            if self.known[e].get(u, 0) < v:
                self.known[e][u] = v
                wl.append((self.streams[s][0], v))
        cs = e
        if dma is not None:
            cs = "d_" + dma
            if cs not in self.streams:
                self._mkstream(cs, 16)
        st = self.streams[cs]
        if signal:
            st[2] += 1
            val = st[2] * st[1]
        else:
            val = (st[2] + 1) * st[1]
        for r in reads:
            ent = self.res.setdefault(r, [None, {}])
            if ent[1].get(cs, 0) < val:
                ent[1][cs] = val
        for w in writes:
            self.res[w] = [(cs, val), {}]
        sem, inc = st[0], st[1]
        rec = {"signal": signal}
        if dma is None:
            self.last[e] = rec

        def emit(eng):
            for sm, v in wl:
                eng.wait_ge(sm, v)
            ins = fn(eng)
            if rec["signal"]:
                ins.then_inc(sem, inc)

        self.ops[e].append(emit)
        self.nops += 1

    def barrier(self):
        for e in ENGS:
            wl = []
            for s, st in self.streams.items():
                v = st[2] * st[1]
                if s != e and self.known[e].get(st[3], 0) < v:
                    self.known[e][st[3]] = v
                    wl.append((st[0], v))

            def emit(eng, wl=wl):
                for sm, v in wl:
                    eng.wait_ge(sm, v)

            self.ops[e].append(emit)

    def finish(self):
        self.barrier()
        nc = self.nc
        ops = self.ops
        with nc.Block() as block:

            @block.tensor
            def _(eng):
                for f in ops["pe"]:
                    f(eng)

            @block.vector
            def _(eng):
                for f in ops["dve"]:
                    f(eng)

            @block.scalar
            def _(eng):
                for f in ops["act"]:
                    f(eng)

            @block.gpsimd
            def _(eng):
                for f in ops["pool"]:
                    f(eng)

            @block.sync
            def _(eng):
                for f in ops["sp"]:
                    f(eng)


class K:
    def __init__(self, S):
        self.S = S
        self.last_rg = None

    def mm(self, out, lhsT, rhs, start, stop, r, w, signal=None, rg=None):
        after = False
        if rg is not None:
            if self.last_rg is not None and rg != self.last_rg:
                self.S.force_signal_last("pe")
                after = True
            self.last_rg = rg
        self.S.op("pe", lambda e: e.matmul(out, lhsT=lhsT, rhs=rhs, start=start, stop=stop),
                  reads=r, writes=w, signal=stop if signal is None else signal, after_self=after)

    def tr(self, out, in_, ident, r, w, signal=True):
        self.S.op("pe", lambda e: e.transpose(out=out, in_=in_, identity=ident),
                  reads=r, writes=w, signal=signal)

    def tt(self, eng, out, a, b, op, r, w):
        self.S.op(eng, lambda e: e.tensor_tensor(out=out, in0=a, in1=b, op=op), reads=r, writes=w)

    def ts(self, eng, out, a, s1, s2, op0, op1, r, w):
        if s2 is None:
            self.S.op(eng, lambda e: e.tensor_scalar(out=out, in0=a, scalar1=s1, scalar2=None, op0=op0),
                      reads=r, writes=w)
        else:
            self.S.op(eng, lambda e: e.tensor_scalar(out=out, in0=a, scalar1=s1, scalar2=s2, op0=op0, op1=op1),
                      reads=r, writes=w)

    def stt(self, out, a, sc, b, op0, op1, r, w):
        self.S.op("dve", lambda e: e.scalar_tensor_tensor(out=out, in0=a, scalar=sc, in1=b, op0=op0, op1=op1),
                  reads=r, writes=w)

    def act(self, out, in_, func, r, w, bias=None, scale=1.0, accum=None):
        kw = {}
        if bias is not None:
            kw["bias"] = bias
        if accum is not None:
            kw["accum_out"] = accum
        self.S.op("act", lambda e: e.activation(out=out, in_=in_, func=func, scale=scale, **kw),
                  reads=r, writes=w)

    def cp(self, eng, out, in_, r, w):
        if eng == "act":
            self.S.op("act", lambda e: e.activation(out=out, in_=in_, func=AF.Copy), reads=r, writes=w)
        else:
            self.S.op(eng, lambda e: e.tensor_copy(out=out, in_=in_), reads=r, writes=w)

    def dma(self, q, out, in_, r, w, stream, **kw):
        self.S.op(q, lambda e: e.dma_start(out=out, in_=in_, **kw), reads=r, writes=w, dma=stream)

    def memset(self, eng, ap, val, w):
        self.S.op(eng, lambda e: e.memset(ap, val), writes=w)


def stage_scope(S):
    es2 = contextlib.ExitStack()
    S._saved_es = S.es
    S.es = es2
    return es2


def stage_end(S, es2):
    S.barrier()
    S.recycle()
    S.es = S._saved_es
    es2.close()


def make_consts(S, k):
    identf = S.sbuf("identf", [128, 128], F32)
    identb = S.sbuf("identb", [128, 128], BF16)
    k.memset("pool", identf[:], 0.0, ["identf"])
    S.op("pool", lambda e: e.affine_select(out=identf[:], in_=identf[:], pattern=[[-1, 128]],
                                           compare_op=ALU.not_equal, fill=1.0, base=0, channel_multiplier=1),
         reads=["identf"], writes=["identf"])
    k.cp("pool", identb[:], identf[:], ["identf"], ["identb"])
    return identf, identb


def layer_norm_block(S, k, z, zkey, out, outkey, gbc, bbc, gbkey, tmp, tag):
    junk, st = tmp["junk"], tmp["st"]
    jk, sk = "junk" + tag, "st" + tag
    k.act(junk[:], z, AF.Copy, [zkey], [jk, sk + "a"], accum=st[:, 0:1])
    k.act(junk[:], z, AF.Square, [zkey], [jk, sk + "b"], accum=st[:, 1:2])
    k.ts("dve", st[:, 2:3], st[:, 0:1], 1.0 / C, None, ALU.mult, None, [sk + "a"], [sk + "c"])
    k.tt("dve", st[:, 3:4], st[:, 2:3], st[:, 2:3], ALU.mult, [sk + "c"], [sk + "d"])
    k.stt(st[:, 4:5], st[:, 1:2], 1.0 / C, st[:, 3:4], ALU.mult, ALU.subtract, [sk + "b", sk + "d"], [sk + "e"])
    k.ts("dve", st[:, 7:8], st[:, 4:5], LN_EPS, None, ALU.add, None, [sk + "e"], [sk + "h"])
    k.act(st[:, 7:8], st[:, 7:8], AF.Sqrt, [sk + "h"], [sk + "h"])
    S.op("dve", lambda e: e.reciprocal(out=st[:, 5:6], in_=st[:, 7:8]), reads=[sk + "h"], writes=[sk + "f"])
    k.stt(st[:, 6:7], st[:, 2:3], -1.0, st[:, 5:6], ALU.mult, ALU.mult, [sk + "c", sk + "f"], [sk + "g"])
    k.act(out, z, AF.Identity, [zkey, sk + "f", sk + "g"], [outkey], bias=st[:, 6:7], scale=st[:, 5:6])
    k.tt("pool", out, out, gbc, ALU.mult, [outkey, gbkey], [outkey])
    k.tt("pool", out, out, bbc, ALU.add, [outkey, gbkey], [outkey])


def mlp_stage(S, k, consts, nb, src, dst, w1d, w2d, lng, lnb, tag):
    identf, identb = consts
    es2 = stage_scope(S)
    w1 = S.sbuf("w1" + tag, [128, 8, DFF], BF16)
    w2 = S.sbuf("w2" + tag, [128, 32, C], BF16)
    gbc = S.sbuf("gbc" + tag, [128, C], F32)
    bbc = S.sbuf("bbc" + tag, [128, C], F32)
    k.dma("sp", gbc[:], lng.partition_broadcast(128), [], ["gb" + tag], "gb" + tag)
    k.dma("sp", bbc[:], lnb.partition_broadcast(128), [], ["gb" + tag], "gb" + tag)
    w1v = w1d.rearrange("(k p) f -> p k f", p=128)
    w2v = w2d.rearrange("(k p) f -> p k f", p=128)
    for kk in range(8):
        k.dma("pool", w1[:, kk, :], w1v[:, kk, :], [], ["w1%s_%d" % (tag, kk)], "w1%s_%d" % (tag, kk))
    for kk in range(32):
        k.dma("pool", w2[:, kk, :], w2v[:, kk, :], [], ["w2%s_%d" % (tag, kk)], "w2%s_%d" % (tag, kk))
    TB = 256
    nblk = nb * 128 // TB
    xb = [S.sbuf("xb%s%d" % (tag, i), [128, 2, C], F32) for i in range(2)]
    xbb = S.sbuf("xbb" + tag, [128, 2, C], BF16)
    xT = S.sbuf("xT" + tag, [128, 8, TB], BF16)
    HT = S.sbuf("HT" + tag, [128, 32, TB], BF16)
    rl = [S.sbuf("rl%s%d" % (tag, i), [128, 512], F32) for i in range(2)]
    zt = [S.sbuf("z%s%d" % (tag, i), [128, C], F32) for i in range(2)]
    ot = [S.sbuf("o%s%d" % (tag, i), [128, C], F32) for i in range(2)]
    junk = S.sbuf("junk" + tag, [128, C], F32)
    sts = [S.sbuf("st%s%d" % (tag, i), [128, 8], F32) for i in range(2)]
    psT = S.psum("psT" + tag, [128, 2, C], BF16)
    psH = [S.psum("psH%s%d" % (tag, i), [128, 512], F32) for i in range(2)]
    psO = [S.psum("psO%s%d" % (tag, i), [128, 512], F32) for i in range(4)]
    srcv = src.rearrange("(n s p) c -> n p s c", p=128, s=2)
    dstv = dst.rearrange("(n s p) c -> n p s c", p=128, s=2)
    for n in range(nblk):
        X = xb[n % 2]
        xk = "xb%s%d" % (tag, n % 2)
        k.dma("sp", X[:], srcv[n], [], [xk], xk)
        k.cp("pool", xbb[:], X[:], [xk], ["xbb" + tag])
        for s in range(2):
            for c in range(8):
                k.tr(psT[:, s, c * 128:(c + 1) * 128], xbb[:, s, c * 128:(c + 1) * 128], identb[:],
                     ["xbb" + tag, "identb"], ["psT%s%d" % (tag, s)], signal=(c == 7))
            k.cp("dve", xT[:, :, s * 128:(s + 1) * 128],
                 psT[:, s, :].rearrange("p (c t) -> p c t", c=8),
                 ["psT%s%d" % (tag, s)], ["xT" + tag])
        for fp in range(16):
            ph = psH[fp % 2]
            pk = "psH%s%d" % (tag, fp % 2)
            for j in range(2):
                f = fp * 2 + j
                for kk in range(8):
                    k.mm(ph[:, j * TB:(j + 1) * TB], w1[:, kk, f * 128:(f + 1) * 128], xT[:, kk, :],
                         kk == 0, kk == 7, ["w1%s_%d" % (tag, kk), "xT" + tag], [pk],
                         signal=(kk == 7 and j == 1))
            R = rl[fp % 2]
            rk = "rl%s%d" % (tag, fp % 2)
            k.act(R[:], ph[:], AF.Relu, [pk], [rk])
            k.tt("pool", HT[:, fp * 2:fp * 2 + 2, :], R[:].rearrange("p (j t) -> p j t", j=2),
                 R[:].rearrange("p (j t) -> p j t", j=2), ALU.mult, [rk], ["HT" + tag])
        for s in range(2):
            Z = zt[s]
            zk = "z%s%d" % (tag, s)
            for h in range(2):
                po = psO[s * 2 + h]
                pk = "psO%s%d" % (tag, s * 2 + h)
                for f in range(32):
                    k.mm(po[:], HT[:, f, s * 128:(s + 1) * 128], w2[:, f, h * 512:(h + 1) * 512],
                         f == 0, f == 31, ["HT" + tag, "w2%s_%d" % (tag, f)], [pk])
                k.stt(Z[:, h * 512:(h + 1) * 512], X[:, s, h * 512:(h + 1) * 512], DN_ALPHA, po[:],
                      ALU.mult, ALU.add, [xk, pk], [zk + "h%d" % h])
            O = ot[s]
            ok = "o%s%d" % (tag, s)
            k.cp("pool", junk[:, 0:1], Z[:, 0:1], [zk + "h0", zk + "h1"], [zk])
            layer_norm_block(S, k, Z[:], zk, O[:], ok, gbc[:], bbc[:], "gb" + tag,
                             {"junk": junk, "st": sts[s]}, "%s%d" % (tag, s))
            k.dma("sp", dstv[n][:, s, :], O[:], [ok], [], "out" + tag + str(s))
    stage_end(S, es2)


DEC_S = -0.6065306597126334
GN_EPS = 64e-5


DBG = {"stop": None}


class _Stop(Exception):
    pass


def rwkv_stage(S, k, consts, nb, xpad, dst, p, s0_in, s_out, lng, lnb, dbg=None):
    identf, identb = consts
    tiles = {}
    es2 = stage_scope(S)

    def T(name, shape, dt):
        t = S.sbuf("rs_" + name, shape, dt)
        tiles[name] = t
        return t

    W = {}
    for nm in ("wr", "wk", "wv", "wo"):
        W[nm] = T(nm, [128, 8, C], BF16)
        k.dma("pool", W[nm][:], p[nm].rearrange("(k p) f -> p k f", p=128), [], [nm], nm)
    W["w1"] = T("w1", [128, 8, 64], BF16)
    W["a1"] = T("a1", [128, 8, 64], BF16)
    W["g1"] = T("g1", [128, 8, 160], BF16)
    for nm in ("w1", "a1", "g1"):
        k.dma("pool", W[nm][:], p[nm].rearrange("(k p) f -> p k f", p=128), [], [nm], nm)
    W["w2"] = T("w2", [64, C], BF16)
    W["a2"] = T("a2", [64, C], BF16)
    W["g2"] = T("g2", [128, 2, C], BF16)
    k.dma("pool", W["w2"][:], p["w2"], [], ["w2"], "w2")
    k.dma("pool", W["a2"][:], p["a2"], [], ["a2"], "a2")
    k.dma("pool", W["g2"][:, 0, :], p["g2"][0:128, :], [], ["g2"], "g2")
    k.dma("pool", W["g2"][0:32, 1, :], p["g2"][128:160, :], [], ["g2"], "g2")
    cols = T("cols", [128, 12, 8], F32)
    for m in range(6):
        k.dma("sp", cols[:, m, :], p["mu"][m].rearrange("(c p) -> p c", p=128), [], ["cols"], "cols",
              allow_slow_non_contiguous=True)
    for i, nm in enumerate(("w0", "a0", "k_k", "k_a", "r_k")):
        k.dma("sp", cols[:, 6 + i, :], p[nm].rearrange("(c p) -> p c", p=128), [], ["cols"], "cols",
              allow_slow_non_contiguous=True)
    k.ts("dve", cols[:, 11, :], cols[:, 9, :], -1.0, 1.0, ALU.mult, ALU.add, ["cols"], ["cols"])
    bc = T("bc", [128, 4, C], F32)
    k.dma("sp", bc[:, 0, :], p["lnx_g"].partition_broadcast(128), [], ["bc"], "bc")
    k.dma("sp", bc[:, 1, :], p["lnx_b"].partition_broadcast(128), [], ["bc"], "bc")
    k.dma("sp", bc[:, 2, :], lng.partition_broadcast(128), [], ["bc"], "bc")
    k.dma("sp", bc[:, 3, :], lnb.partition_broadcast(128), [], ["bc"], "bc")

    def colbc(i):
        return cols[:, i, :].unsqueeze(2).to_broadcast([128, 8, 128])

    mL = T("mL", [128, 128], F32)
    mU = T("mU", [128, 128], F32)
    mUd = T("mUd", [128, 128], F32)
    for t_, key, cmp_, base in ((mL, "mL", ALU.is_gt, 0), (mU, "mU", ALU.is_gt, 0), (mUd, "mUd", ALU.is_ge, 0)):
        k.memset("pool", t_[:], 1.0, [key])
    S.op("pool", lambda e: e.affine_select(out=mL[:], in_=mL[:], pattern=[[-1, 128]], compare_op=ALU.is_gt,
                                           fill=0.0, base=0, channel_multiplier=1), reads=["mL"], writes=["mL"])
    S.op("pool", lambda e: e.affine_select(out=mU[:], in_=mU[:], pattern=[[1, 128]], compare_op=ALU.is_gt,
                                           fill=0.0, base=0, channel_multiplier=-1), reads=["mU"], writes=["mU"])
    S.op("pool", lambda e: e.affine_select(out=mUd[:], in_=mUd[:], pattern=[[1, 128]], compare_op=ALU.is_ge,
                                           fill=0.0, base=0, channel_multiplier=-1), reads=["mUd"], writes=["mUd"])
    bones = T("bones", [128, 128], BF16)
    k.memset("pool", bones[:], 0.0, ["bones"])
    k.memset("pool", bones[0:64, 0:64], 1.0, ["bones"])
    k.memset("pool", bones[64:128, 64:128], 1.0, ["bones"])
    rmask = T("rmask", [128, 8, 128], F32)
    k.memset("pool", rmask[:], 1.0, ["rmask"])
    k.memset("pool", rmask[:, :, 0:1], 0.0, ["rmask"])

    def mbc(m, n):
        return m[:].unsqueeze(1).to_broadcast([128, n, 128])

    f = [T("f%d" % i, [128, 8, 128], F32) for i in range(11)]
    b = [T("b%d" % i, [128, 8, 128], BF16) for i in range(14)]
    fk = ["f%d" % i for i in range(11)]
    bk = ["b%d" % i for i in range(14)]
    Pg = [T("Pg%d" % i, [128, 4, 128], BF16) for i in range(2)]
    PTg = [T("PTg%d" % i, [128, 4, 128], BF16) for i in range(2)]
    STg = [T("STg%d" % i, [128, 4, 128], BF16) for i in range(2)]
    TT = T("TT", [128, 16, 128], BF16)
    AakT = T("AakT", [128, 16, 128], BF16)
    ArbT = T("ArbT", [128, 16, 128], BF16)
    ArkT = T("ArkT", [128, 16, 128], BF16)
    Sf = T("Sf", [128, 8, 64], F32)
    Sb = T("Sb", [128, 8, 64], BF16)
    pl = T("pl", [128, 8], F32)
    sm = T("sm", [128, 8, 16], F32)
    st = T("st", [128, 8], F32)
    junk = T("junk", [128, C], F32)
    pA = S.psum("rw_pA", [128, 1024], F32)
    pB = S.psum("rw_pB", [128, 1024], F32)
    pC = S.psum("rw_pC", [128, 1024], F32)
    pT = S.psum("rw_pT", [128, 2, 1024], BF16)

    def v3(ap):
        return ap.rearrange("p (c t) -> p c t", c=8)

    def flat(t):
        return t[:].rearrange("p c t -> p (c t)")

    k.dma("sp", Sf[:], s0_in, [], ["Sf"], "Sf")
    k.cp("dve", Sb[:], Sf[:], ["Sf"], ["Sb"])

    ones_col = T("ones_col", [128, 1], BF16)
    k.memset("pool", ones_col[:], 1.0, ["ones_col"])

    def chk(name, ap, key):
        if DBG["stop"] == name:
            k.dma("sp", dst[0:128, :], ap, [key], [], "rwout")
            raise _Stop()

    chk("setup", bc[:, 0, :], "bc")
    HO = [0, 2, 4, 6, 8, 10, 12, 14, 1, 3, 5, 7, 9, 11, 13, 15]
    for i in range(nb):
      try:
        k.dma("sp", flat(f[0]), xpad[1 + 128 * i:1 + 128 * (i + 1), :], [], [fk[0]], "xc")
        k.dma("sp", flat(f[1]), xpad[128 * i:128 * (i + 1), :], [], [fk[1]], "xp")
        for c in range(8):
            k.tr(pA[:, c * 128:(c + 1) * 128], flat(f[0])[:, c * 128:(c + 1) * 128], identf[:], [fk[0], "identf"], ["pA"], signal=(c == 7))
        for c in range(8):
            k.tr(pB[:, c * 128:(c + 1) * 128], flat(f[1])[:, c * 128:(c + 1) * 128], identf[:], [fk[1], "identf"], ["pB"], signal=(c == 7))
        k.cp("act", flat(f[2]), pA[:], ["pA"], [fk[2]])
        k.tt("dve", flat(f[3]), pB[:], flat(f[2]), ALU.subtract, ["pB", fk[2]], [fk[3]])
        for m in range(6):
            k.tt("pool", f[4][:], f[3][:], colbc(m), ALU.mult, [fk[3], "cols"], [fk[4]])
            k.tt("pool", b[m][:], f[4][:], f[2][:], ALU.add, [fk[4], fk[2]], [bk[m]])
        xr, xw, xk_, xv, xa, xg = b[0], b[1], b[2], b[3], b[4], b[5]
        chk("mix", flat(f[4]), fk[4])
        for c in range(8):
            for kk in range(8):
                k.mm(pA[:, c * 128:(c + 1) * 128], W["wr"][:, kk, c * 128:(c + 1) * 128], xr[:, kk, :], kk == 0, kk == 7,
                     ["wr", bk[0]], ["pA"], signal=(kk == 7 and c == 7))
        for c in range(8):
            for kk in range(8):
                k.mm(pB[:, c * 128:(c + 1) * 128], W["wk"][:, kk, c * 128:(c + 1) * 128], xk_[:, kk, :], kk == 0, kk == 7,
                     ["wk", bk[2]], ["pB"], signal=(kk == 7 and c == 7))
        for h in range(2):
            for kk in range(8):
                k.mm(pC[:, h * 512:(h + 1) * 512], xv[:, kk, :], W["wv"][:, kk, h * 512:(h + 1) * 512], kk == 0, kk == 7,
                     ["wv", bk[3]], ["pC"], signal=(kk == 7 and h == 1))
        k.cp("act", flat(f[5]), pA[:], ["pA"], [fk[5]])
        k.cp("dve", flat(f[6]), pB[:], ["pB"], [fk[6]])
        k.cp("act", flat(b[6]), pC[:], ["pC"], [bk[6]])
        V = flat(b[6])
        chk("proj", flat(f[5]), fk[5])
        for kk in range(8):
            k.mm(pA[0:64, 0:128], W["w1"][:, kk, :], xw[:, kk, :], kk == 0, kk == 7, ["w1", bk[1]], ["pA"], signal=False)
        for kk in range(8):
            k.mm(pA[0:64, 128:256], W["a1"][:, kk, :], xa[:, kk, :], kk == 0, kk == 7, ["a1", bk[4]], ["pA"], signal=False)
        for kk in range(8):
            k.mm(pA[:, 256:384], W["g1"][:, kk, 0:128], xg[:, kk, :], kk == 0, kk == 7, ["g1", bk[5]], ["pA"], signal=False)
        for kk in range(8):
            k.mm(pA[0:32, 384:512], W["g1"][:, kk, 128:160], xg[:, kk, :], kk == 0, kk == 7, ["g1", bk[5]], ["pA"])
        hb = flat(b[7])
        k.act(hb[0:64, 0:128], pA[0:64, 0:128], AF.Tanh, ["pA"], [bk[7]])
        k.act(hb[0:64, 128:256], pA[0:64, 128:256], AF.Copy, ["pA"], [bk[7]])
        k.act(hb[:, 256:384], pA[:, 256:384], AF.Sigmoid, ["pA"], [bk[7]])
        k.act(hb[0:32, 384:512], pA[0:32, 384:512], AF.Sigmoid, ["pA"], [bk[7]])
        for c in range(8):
            k.mm(pB[:, c * 128:(c + 1) * 128], W["w2"][:, c * 128:(c + 1) * 128], hb[0:64, 0:128], True, True, ["w2", bk[7]], ["pB"], signal=(c == 7))
        for c in range(8):
            k.mm(pC[:, c * 128:(c + 1) * 128], W["a2"][:, c * 128:(c + 1) * 128], hb[0:64, 128:256], True, True, ["a2", bk[7]], ["pC"], signal=(c == 7))
        k.tt("dve", f[7][:], v3(pB[:]), colbc(6), ALU.add, ["pB", "cols"], [fk[7]])
        k.act(f[7][:], f[7][:], AF.Sigmoid, [fk[7]], [fk[7]])
        k.tt("dve", f[8][:], v3(pC[:]), colbc(7), ALU.add, ["pC", "cols"], [fk[8]])
        k.act(f[8][:], f[8][:], AF.Sigmoid, [fk[8]], [fk[8]])
        chk("lora", flat(f[8]), fk[8])
        k.tt("pool", f[9][:], f[6][:], colbc(8), ALU.mult, [fk[6], "cols"], [fk[9]])
        k.tt("pool", b[8][:], f[9][:], f[9][:], ALU.mult, [fk[9]], [bk[8]])
        for c in range(8):
            k.mm(pA[:, c * 128:(c + 1) * 128], bones[:], b[8][:, c, :], True, True, ["bones", bk[8]], ["pA"], signal=(c == 7))
        k.ts("dve", flat(f[4]), pA[:], 1e-24, None, ALU.max, None, ["pA"], [fk[4]])
        k.act(f[4][:], f[4][:], AF.Sqrt, [fk[4]], [fk[4]])
        S.op("dve", lambda e: e.reciprocal(out=f[4][:], in_=f[4][:]), reads=[fk[4]], writes=[fk[4]])
        k.tt("pool", f[9][:], f[9][:], f[4][:], ALU.mult, [fk[9], fk[4]], [fk[9]])
        k.tt("pool", f[4][:], f[8][:], colbc(9), ALU.mult, [fk[8], "cols"], [fk[4]])
        k.tt("pool", f[4][:], f[4][:], colbc(11), ALU.add, [fk[4], "cols"], [fk[4]])
        k.tt("pool", f[6][:], f[6][:], f[4][:], ALU.mult, [fk[6], fk[4]], [fk[6]])
        k.tt("pool", f[4][:], f[9][:], f[8][:], ALU.mult, [fk[9], fk[8]], [fk[4]])
        k.tt("dve", f[10][:], f[5][:], f[6][:], ALU.mult, [fk[5], fk[6]], [fk[10]])
        k.tt("pool", b[9][:], f[10][:], colbc(10), ALU.mult, [fk[10], "cols"], [bk[9]])
        chk("kk", flat(f[9]), fk[9])
        S.op("dve", lambda e: e.tensor_tensor_scan(out=flat(f[10]), data0=flat(rmask), data1=flat(f[7]), initial=0.0,
                                                   op0=ALU.mult, op1=ALU.add), reads=["rmask", fk[7]], writes=[fk[10]])
        cs = f[10]
        k.act(f[2][:], cs[:], AF.Exp, [fk[10]], [fk[2]], scale=DEC_S)
        k.tt("dve", b[10][:], f[5][:], f[2][:], ALU.mult, [fk[5], fk[2]], [bk[10]])
        k.tt("pool", f[3][:], cs[:], f[7][:], ALU.subtract, [fk[10], fk[7]], [fk[3]])
        k.act(f[3][:], f[3][:], AF.Exp, [fk[3]], [fk[3]], scale=DEC_S)
        k.stt(flat(b[11]), flat(f[9]), -1.0, flat(f[3]), ALU.mult, ALU.mult, [fk[9], fk[3]], [bk[11]])
        k.act(f[2][:], cs[:], AF.Exp, [fk[10], bk[10]], [fk[2]], scale=-DEC_S)
        k.tt("dve", b[12][:], f[4][:], f[2][:], ALU.mult, [fk[4], fk[2]], [bk[12]])
        k.tt("pool", b[13][:], f[6][:], f[2][:], ALU.mult, [fk[6], fk[2]], [bk[13]])
        csL = cs[:, :, 127:128]
        k.tt("pool", f[3][:], cs[:], csL.to_broadcast([128, 8, 128]), ALU.subtract, [fk[10], bk[11]], [fk[3]])
        k.act(f[3][:], f[3][:], AF.Exp, [fk[3]], [fk[3]], scale=-DEC_S)
        k.tt("dve", b[1][:], f[4][:], f[3][:], ALU.mult, [fk[4], fk[3]], [bk[1]])
        k.tt("pool", b[2][:], f[6][:], f[3][:], ALU.mult, [fk[6], fk[3]], [bk[2]])
        k.act(pl[:], cs[:, :, 127], AF.Exp, [fk[10]], ["pl"], scale=DEC_S)
        chk("prep", flat(f[3]), fk[3])
        for j, src in enumerate((1, 2)):
            for c in range(8):
                k.tr(pT[:, j, c * 128:(c + 1) * 128], b[src][:, c, :], identb[:], [bk[src], "identb"], ["pT%d" % j], signal=(c == 7))
        k.cp("act", flat(b[3]), pT[:, 0, :], ["pT0"], [bk[3]])
        k.cp("dve", flat(b[4]), pT[:, 1, :], ["pT1"], [bk[4]])
        Bh, Kh = flat(b[3]), flat(b[4])
        rt, at, bt, kt = b[10], b[11], b[12], b[13]
        chk("trB", flat(f[3]), fk[3])
        for gq in range(4):
            for hh in (0, 2, 1, 3):
                h = gq * 4 + hh
                c, ba = h // 2, 64 * (h % 2)
                sl = slice(ba, ba + 64)
                rd = [bk[10], bk[11], bk[12], bk[13]]
                k.mm(pA[:, hh * 128:(hh + 1) * 128], at[sl, c, :], bt[sl, c, :], True, True, rd, ["pA"], signal=False, rg=ba)
                k.mm(pB[:, hh * 128:(hh + 1) * 128], bt[sl, c, :], at[sl, c, :], True, True, rd, ["pB"], signal=False, rg=ba)
                k.mm(pB[:, 512 + hh * 128:512 + (hh + 1) * 128], bt[sl, c, :], rt[sl, c, :], True, True, rd, ["pB"], signal=False, rg=ba)
                k.mm(pC[:, hh * 128:(hh + 1) * 128], kt[sl, c, :], at[sl, c, :], True, True, rd, ["pC"], signal=False, rg=ba)
                k.mm(pC[:, 512 + hh * 128:512 + (hh + 1) * 128], kt[sl, c, :], rt[sl, c, :], True, True, rd, ["pC"], signal=(hh == 3), rg=ba)
            g4 = lambda ap: ap.rearrange("p (h t) -> p h t", h=4)
            k.tt("dve", Pg[0][:], g4(pA[:, 0:512]), mbc(mL, 4), ALU.mult, ["pA", "mL"], ["Pg0"])
            k.tt("dve", PTg[0][:], g4(pB[:, 0:512]), mbc(mU, 4), ALU.mult, ["pB", "mU"], ["PTg0"])
            k.tt("dve", ArbT[:, gq * 4:gq * 4 + 4, :], g4(pB[:, 512:1024]), mbc(mUd, 4), ALU.mult, ["pB", "mUd"], ["ArbT"])
            k.tt("dve", AakT[:, gq * 4:gq * 4 + 4, :], g4(pC[:, 0:512]), mbc(mU, 4), ALU.mult, ["pC", "mU"], ["AakT"])
            k.tt("dve", ArkT[:, gq * 4:gq * 4 + 4, :], g4(pC[:, 512:1024]), mbc(mUd, 4), ALU.mult, ["pC", "mUd"], ["ArkT"])
            k.tt("pool", STg[0][:], PTg[0][:], identb[:].unsqueeze(1).to_broadcast([128, 4, 128]), ALU.add, ["PTg0", "identb"], ["STg0"])
            if gq == 0:
                chk("blk0", flat(f[3]), fk[3])
            cur = 0
            for lv in range(6):
                nx = 1 - cur
                for hh in range(4):
                    k.mm(pA[:, hh * 128:(hh + 1) * 128], PTg[cur][:, hh, :], Pg[cur][:, hh, :], True, True,
                         ["Pg%d" % cur, "PTg%d" % cur], ["pA"], signal=(hh == 3))
                if lv < 5:
                    for hh in range(4):
                        k.mm(pB[:, hh * 128:(hh + 1) * 128], Pg[cur][:, hh, :], PTg[cur][:, hh, :], True, True,
                             ["Pg%d" % cur, "PTg%d" % cur], ["pB"], signal=(hh == 3))
                k.cp("act", Pg[nx][:], g4(pA[:, 0:512]), ["pA"], ["Pg%d" % nx])
                if lv < 5:
                    k.cp("dve", PTg[nx][:], g4(pB[:, 0:512]), ["pB"], ["PTg%d" % nx])
                for hh in range(4):
                    k.mm(pC[:, hh * 128:(hh + 1) * 128], identb[:], STg[cur][:, hh, :], True, False,
                         ["identb", "STg%d" % cur], ["pC"], signal=False)
                    k.mm(pC[:, hh * 128:(hh + 1) * 128], Pg[nx][:, hh, :], STg[cur][:, hh, :], False, True,
                         ["Pg%d" % nx, "STg%d" % cur], ["pC"], signal=(hh == 3))
                if lv < 5:
                    k.cp("act" if lv % 2 else "dve", STg[nx][:], g4(pC[:, 0:512]), ["pC"], ["STg%d" % nx])
                else:
                    k.cp("dve", TT[:, gq * 4:gq * 4 + 4, :], g4(pC[:, 0:512]), ["pC"], ["TT"])
                cur = nx
        chk("blk", flat(f[3]), fk[3])
        for h in range(16):
            k.mm(pA[:, h * 64:(h + 1) * 64], AakT[:, h, :], V[:, h * 64:(h + 1) * 64], True, True, ["AakT", bk[6]], ["pA"], signal=(h == 15))
        k.cp("act", flat(f[1]), pA[:], ["pA"], [fk[1]])
        for h in HO:
            c, ba = h // 2, 64 * (h % 2)
            k.mm(pC[:, h:h + 1], b[9][ba:ba + 64, c, :], ones_col[ba:ba + 64, :], True, True, [bk[9], "ones_col"], ["pC"], signal=(h == 15), rg=ba)
        k.cp("dve", sm[:, 0, :], pC[:, 0:16], ["pC"], ["sm"])
        for h in HO:
            c, ba = h // 2, 64 * (h % 2)
            k.mm(pB[:, h * 64:(h + 1) * 64], at[ba:ba + 64, c, :], Sb[ba:ba + 64, c, :], True, True, [bk[11], "Sb"], ["pB"], signal=(h == 15), rg=ba)
        k.tt("dve", flat(b[8]), pB[:], flat(f[1]), ALU.add, ["pB", fk[1]], [bk[8]])
        Z = flat(b[8])
        for h in range(16):
            k.mm(pC[:, h * 64:(h + 1) * 64], TT[:, h, :], Z[:, h * 64:(h + 1) * 64], True, True, ["TT", bk[8]], ["pC"], signal=(h == 15))
        k.cp("act", flat(b[0]), pC[:], ["pC"], [bk[0]])
        U = flat(b[0])
        for h in HO:
            c, ba = h // 2, 64 * (h % 2)
            hs = slice(h * 64, (h + 1) * 64)
            k.mm(pA[:, hs], ArkT[:, h, :], V[:, hs], True, False, ["ArkT", bk[6]], ["pA"], signal=False)
            k.mm(pA[:, hs], ArbT[:, h, :], U[:, hs], False, False, ["ArbT", bk[0]], ["pA"], signal=False)
            k.mm(pA[:, hs], rt[ba:ba + 64, c, :], Sb[ba:ba + 64, c, :], False, True, [bk[10], "Sb"], ["pA"], signal=(h == 15), rg=ba)
        for h in HO:
            c, ba = h // 2, 64 * (h % 2)
            hs = slice(h * 64, (h + 1) * 64)
            k.mm(pB[ba:ba + 64, c * 64:(c + 1) * 64], Bh[:, hs], U[:, hs], True, False, [bk[3], bk[0]], ["pB"], signal=False, rg=100 + ba)
            k.mm(pB[ba:ba + 64, c * 64:(c + 1) * 64], Kh[:, hs], V[:, hs], False, True, [bk[4], bk[6]], ["pB"], signal=(h == 15), rg=100 + ba)
        k.tt("pool", Sf[:], Sf[:], pl[:].unsqueeze(2).to_broadcast([128, 8, 64]), ALU.mult, ["Sf", "pl"], ["Sf"])
        k.tt("dve", Sf[:], Sf[:], pB[:, 0:512].rearrange("p (c v) -> p c v", c=8), ALU.add, ["Sf", "pB"], ["Sf"])
        k.cp("act", Sb[:], Sf[:], ["Sf"], ["Sb"])
        chk("chain", flat(f[1]), fk[1])
        o3 = lambda ap: ap.rearrange("p (h v) -> p h v", h=16)
        k.cp("act", flat(f[5]), pA[:], ["pA"], [fk[5]])
        O = flat(f[5])
        S.op("dve", lambda e: e.tensor_reduce(out=sm[:, 1, :], in_=o3(O), axis=AX.X, op=ALU.add), reads=[fk[5]], writes=["sm"])
        k.ts("dve", sm[:, 1, :], sm[:, 1, :], 1.0 / 64, None, ALU.mult, None, ["sm"], ["sm"])
        k.tt("pool", o3(O), o3(O), sm[:, 1, :].unsqueeze(2).to_broadcast([128, 16, 64]), ALU.subtract, [fk[5], "sm"], [fk[5]])
        k.tt("pool", flat(f[6]), O, O, ALU.mult, [fk[5]], [fk[6]])
        S.op("dve", lambda e: e.tensor_reduce(out=sm[:, 2, :], in_=o3(flat(f[6])), axis=AX.X, op=ALU.add), reads=[fk[6]], writes=["sm"])
        k.ts("dve", sm[:, 2, :], sm[:, 2, :], 1.0 / 64, GN_EPS, ALU.mult, ALU.add, ["sm"], ["sm"])
        k.act(sm[:, 2, :], sm[:, 2, :], AF.Sqrt, ["sm"], ["sm"])
        S.op("dve", lambda e: e.reciprocal(out=sm[:, 2, :], in_=sm[:, 2, :]), reads=["sm"], writes=["sm"])
        k.tt("pool", o3(O), o3(O), sm[:, 2, :].unsqueeze(2).to_broadcast([128, 16, 64]), ALU.mult, [fk[5], "sm"], [fk[5]])
        k.tt("pool", O, O, bc[:, 0, :], ALU.mult, [fk[5], "bc"], [fk[5]])
        k.tt("pool", O, O, bc[:, 1, :], ALU.add, [fk[5], "bc"], [fk[5]])
        k.tt("pool", o3(flat(f[6])), o3(V), sm[:, 0, :].unsqueeze(2).to_broadcast([128, 16, 64]), ALU.mult, [bk[6], "sm"], [fk[6]])
        k.tt("pool", O, O, flat(f[6]), ALU.add, [fk[5], fk[6]], [fk[5]])
        for hf in range(2):
            k.mm(pC[:, hf * 512:(hf + 1) * 512], hb[:, 256:384], W["g2"][:, 0, hf * 512:(hf + 1) * 512], True, False,
                 [bk[7], "g2"], ["pC"], signal=False)
            k.mm(pC[:, hf * 512:(hf + 1) * 512], hb[0:32, 384:512], W["g2"][0:32, 1, hf * 512:(hf + 1) * 512], False, True,
                 [bk[7], "g2"], ["pC"], signal=(hf == 1))
        k.tt("dve", flat(b[9]), O, pC[:], ALU.mult, [fk[5], "pC"], [bk[9]])
        for c in range(8):
            k.tr(pT[:, 0, c * 128:(c + 1) * 128], flat(b[9])[:, c * 128:(c + 1) * 128], identb[:], [bk[9], "identb"], ["pT0"], signal=(c == 7))
        k.cp("act", flat(b[8]), pT[:, 0, :], ["pT0"], [bk[8]])
        for hf in range(2):
            for kk in range(8):
                k.mm(pC[:, hf * 512:(hf + 1) * 512], b[8][:, kk, :], W["wo"][:, kk, hf * 512:(hf + 1) * 512], kk == 0, kk == 7,
                     [bk[8], "wo"], ["pC"], signal=(kk == 7 and hf == 1))
        k.stt(flat(f[6]), flat(f[0]), DN_ALPHA, pC[:], ALU.mult, ALU.add, [fk[0], "pC"], [fk[6]])
        layer_norm_block(S, k, flat(f[6]), fk[6], flat(f[9]), fk[9], bc[:, 2, :], bc[:, 3, :], "bc",
                         {"junk": junk, "st": st}, "rw")
        k.dma("sp", dst[128 * i:128 * (i + 1), :], flat(f[9]), [fk[9]], [], "rwout")
        if dbg is not None and i == 0:
            pass
      except _Stop:
        break
    k.dma("sp", s_out, Sf[:], ["Sf"], [], "sout")
    stage_end(S, es2)


def s5_stage(S, k, consts, nb, src, dst, p, h_in, h_out, lng, lnb):
    identf, identb = consts
    es2 = stage_scope(S)

    def T(name, shape, dt):
        return S.sbuf("s5_" + name, shape, dt)

    PI2 = float(np.pi / 2)
    sm = T("sm", [128, 20, 32], F32)
    smk = ["sm%d" % i for i in range(20)]
    hp = T("hp", [128, 1], F32)
    k.memset("pool", hp[:], PI2, ["hp"])
    Cx = [T("Cx%d" % i, [128, 32, 16], BF16) for i in range(2)]
    BTt = T("BTt", [128, 2, 8, 2, 128], BF16)
    CT = T("CT", [128, 32, 128], BF16)
    ST = T("ST", [128, 32, 128], BF16)
    c128 = T("c128", [128, 32], F32)
    s128 = T("s128", [128, 32], F32)
    carry = T("carry", [128, 32, 2], F32)
    cst = T("cst", [128, 6, 8], F32)
    wglu = T("wglu", [128, 8, 2 * C], BF16)
    bc = T("bc", [128, 3, C], F32)
    es_setup = contextlib.ExitStack()
    S.es = es_setup
    Bt = [T("Bt%d" % i, [128, 32, 16], F32) for i in range(2)]
    Bb = [T("Bb%d" % i, [128, 32, 16], F32) for i in range(2)]
    BA = T("BA", [128, 32, 16], F32)
    BB = T("BB", [128, 32, 16], F32)
    CN = [T("CN%d" % i, [128, 8, 64], F32) for i in range(2)]
    Wsrc = [[T("Ws%d%d" % (v, ri), [128, 128], F32) for ri in range(2)] for v in range(2)]
    CTf = T("CTf", [128, 32, 128], F32)
    STf = T("STf", [128, 32, 128], F32)
    tA = T("tA", [128, 32, 64], F32)
    tB = T("tB", [128, 32, 64], F32)
    S.es = es2
    pU = S.psum("s5_pU", [128, 2, 1024], F32)
    pY = S.psum("s5_pY", [128, 1024], F32)
    pT = S.psum("s5_pT", [128, 2, 1024], BF16)

    for kk in range(8):
        k.dma("pool", wglu[:, kk, :], p["w_glu"].rearrange("(k p) f -> p k f", p=128)[:, kk, :], [], ["wglu"], "wglu")
    k.dma("sp", bc[:, 0, :], p["d"].partition_broadcast(128), [], ["bc"], "bc")
    k.dma("sp", bc[:, 1, :], lng.partition_broadcast(128), [], ["bc"], "bc")
    k.dma("sp", bc[:, 2, :], lnb.partition_broadcast(128), [], ["bc"], "bc")
    k.dma("sp", carry[:], h_in, [], ["carry"], "carry")
    for g2 in range(2):
        sl = slice(64 * g2, 64 * g2 + 64)
        k.dma("sp", sm[sl, 0, :], p["a_re"].rearrange("(j t) p -> t p j", t=2)[g2], [], [smk[0]], "sm0", allow_slow_non_contiguous=True)
        k.dma("sp", sm[sl, 1, :], p["a_im"].rearrange("(j t) p -> t p j", t=2)[g2], [], [smk[1]], "sm1", allow_slow_non_contiguous=True)
        k.dma("sp", sm[sl, 2, :], p["log_dt"].rearrange("o (j t) -> t o j", t=2)[g2].partition_broadcast(64), [], [smk[2]], "sm2",
              allow_slow_non_contiguous=True)
        for ri, nm in enumerate(("b_re", "b_im")):
            k.dma("sp", Bt[ri][sl, :, :], p[nm].rearrange("(j t) p c -> t p j c", t=2)[g2], [], ["Bt%d" % ri], "Bt%d" % ri)
    for ri, nm in enumerate(("c_re", "c_im")):
        k.dma("sp", CN[ri][:], p[nm].rearrange("(t g) c p -> (g c) t p", g=8), [], ["CN%d" % ri], "CN%d" % ri)

    def sl_(i):
        return sm[:, i, :]

    def stt_s(o, a, b_, op):
        k.tt("dve", sl_(o), sl_(a), sl_(b_), op, [smk[a], smk[b_]], [smk[o]])

    k.act(sl_(2), sl_(2), AF.Exp, [smk[2]], [smk[2]])
    k.ts("dve", sl_(0), sl_(0), -1e-4, None, ALU.min, None, [smk[0]], [smk[0]])
    stt_s(3, 2, 1, ALU.mult)
    stt_s(4, 2, 0, ALU.mult)
    k.act(sl_(4), sl_(4), AF.Exp, [smk[4]], [smk[4]])
    k.act(sl_(5), sl_(3), AF.Sin, [smk[3]], [smk[5]], scale=1.0 / 32)
    k.act(sl_(6), sl_(3), AF.Sin, [smk[3], "hp"], [smk[6]], bias=hp[:, 0:1], scale=1.0 / 32)

    def csq(ci_, si_):
        stt_s(7, ci_, ci_, ALU.mult)
        stt_s(8, si_, si_, ALU.mult)
        k.stt(sl_(si_), sl_(ci_), 2.0, sl_(si_), ALU.mult, ALU.mult, [smk[ci_], smk[si_]], [smk[si_]])
        stt_s(ci_, 7, 8, ALU.subtract)

    for _ in range(5):
        csq(6, 5)
    stt_s(9, 4, 6, ALU.mult)
    stt_s(10, 4, 5, ALU.mult)
    k.ts("dve", sl_(11), sl_(9), -1.0, None, ALU.add, None, [smk[9]], [smk[11]])
    stt_s(12, 0, 0, ALU.mult)
    stt_s(15, 1, 1, ALU.mult)
    stt_s(12, 12, 15, ALU.add)
    S.op("dve", lambda e: e.reciprocal(out=sl_(12), in_=sl_(12)), reads=[smk[12]], writes=[smk[12]])
    stt_s(13, 11, 0, ALU.mult)
    stt_s(15, 10, 1, ALU.mult)
    stt_s(13, 13, 15, ALU.add)
    stt_s(13, 13, 12, ALU.mult)
    stt_s(14, 10, 0, ALU.mult)
    stt_s(15, 11, 1, ALU.mult)
    stt_s(14, 14, 15, ALU.subtract)
    stt_s(14, 14, 12, ALU.mult)

    def bc16(i):
        return sm[:, i, :].unsqueeze(2).to_broadcast([128, 32, 16])

    k.tt("dve", BA[:], Bt[0][:], bc16(13), ALU.mult, ["Bt0", smk[13]], ["BA"])
    k.tt("dve", BB[:], Bt[1][:], bc16(14), ALU.mult, ["Bt1", smk[14]], ["BB"])
    k.tt("dve", Bb[0][:], BA[:], BB[:], ALU.subtract, ["BA", "BB"], ["Bb0"])
    k.tt("dve", BA[:], Bt[0][:], bc16(14), ALU.mult, ["Bt0", smk[14]], ["BA"])
    k.tt("dve", BB[:], Bt[1][:], bc16(13), ALU.mult, ["Bt1", smk[13]], ["BB"])
    k.tt("dve", Bb[1][:], BA[:], BB[:], ALU.add, ["BA", "BB"], ["Bb1"])
    for v in range(2):
        for ri in range(2):
            k.memset("pool", Wsrc[v][ri][:], 0.0, ["Ws%d%d" % (v, ri)])
    for c in range(8):
        for v in range(2):
            for ri in range(2):
                W_ = Wsrc[v][ri]
                wk_ = "Ws%d%d" % (v, ri)
                wv = W_[:].rearrange("p (jh jv t c) -> p jh jv t c", jh=2, jv=2, t=2, c=16)
                src_ = Bb[ri][:, 4 * c:4 * c + 4, :].rearrange("p (jh jv) c -> p jh jv c", jv=2)
                k.cp("pool", wv[0:64, :, v, 0, :], src_[0:64, :, v, :], ["Bb%d" % ri], [wk_])
                k.cp("pool", wv[64:128, :, v, 1, :], src_[64:128, :, v, :], ["Bb%d" % ri], [wk_])
                k.tr(pY[:, 0:128], W_[:], identf[:], [wk_, "identf"], ["pY"])
                k.cp("dve", BTt[:, v, c, ri, :], pY[:, 0:128], ["pY"], ["BTt"])
    for ri in range(2):
        for t in range(8):
            k.tr(pU[0:64, ri, t * 128:(t + 1) * 128], CN[ri][:, t, :], identf[:], ["CN%d" % ri, "identf"], ["pU"], signal=(t == 7))
        pv = pU[0:64, ri, :].rearrange("p (j h c) -> p j h c", j=32, h=2, c=16)
        sc = 1.0 if ri == 0 else -1.0
        k.ts("dve", Cx[ri][0:64, :, :], pv[:, :, 0, :], sc, None, ALU.mult, None, ["pU"], ["Cx%d" % ri])
        k.ts("dve", Cx[ri][64:128, :, :], pv[:, :, 1, :], sc, None, ALU.mult, None, ["pU"], ["Cx%d" % ri])
    k.cp("dve", CTf[:, :, 0], sl_(6), [smk[6]], ["CTf"])
    k.cp("dve", STf[:, :, 0], sl_(5), [smk[5]], ["STf"])
    k.cp("dve", sl_(16), sl_(6), [smk[6]], [smk[16]])
    k.cp("dve", sl_(17), sl_(5), [smk[5]], [smk[17]])
    for lv in range(7):
        n = 1 << lv
        cwb = sm[:, 16, :].unsqueeze(2).to_broadcast([128, 32, n])
        swb = sm[:, 17, :].unsqueeze(2).to_broadcast([128, 32, n])
        k.tt("dve", tA[:, :, 0:n], CTf[:, :, 0:n], cwb, ALU.mult, ["CTf", smk[16]], ["tA"])
        k.tt("dve", tB[:, :, 0:n], STf[:, :, 0:n], swb, ALU.mult, ["STf", smk[17]], ["tB"])
        k.tt("dve", CTf[:, :, n:2 * n], tA[:, :, 0:n], tB[:, :, 0:n], ALU.subtract, ["tA", "tB"], ["CTf2"])
        k.tt("dve", tA[:, :, 0:n], STf[:, :, 0:n], cwb, ALU.mult, ["STf", smk[16]], ["tA"])
        k.tt("dve", tB[:, :, 0:n], CTf[:, :, 0:n], swb, ALU.mult, ["CTf", smk[17]], ["tB"])
        k.tt("dve", STf[:, :, n:2 * n], tA[:, :, 0:n], tB[:, :, 0:n], ALU.add, ["tA", "tB", "CTf2"], ["STf", "CTf"])
        if lv < 6:
            csq(16, 17)
    k.cp("dve", c128[:], CTf[:, :, 127], ["CTf", "STf"], ["c128"])
    k.cp("dve", s128[:], STf[:, :, 127], ["CTf", "STf"], ["s128"])
    k.cp("pool", CT[:], CTf[:], ["CTf", "STf"], ["CT"])
    k.cp("pool", ST[:], STf[:], ["CTf", "STf"], ["ST"])

    S.barrier()
    es_setup.close()
    X = [T("X%d" % i, [128, C], F32) for i in range(2)]
    xbb = T("xbb", [128, C], BF16)
    uT = T("uT", [128, 8, 128], BF16)
    wt = [T("wt%d" % i, [128, 8, 128], F32) for i in range(6)]
    gs = [T("gs%d" % i, [128, 8, 128], F32) for i in range(2)]
    hb = [T("hb%d" % i, [128, 8, 128], BF16) for i in range(2)]
    Y = T("Y", [128, C], F32)
    Y2 = T("Y2", [128, C], F32)
    geb = T("geb", [128, C], BF16)
    yT = T("yT", [128, 8, 128], BF16)
    sgl = T("sgl", [128, C], F32)
    Z = T("Z", [128, C], F32)
    O = T("O", [128, C], F32)
    junk = T("junk", [128, C], F32)
    st = T("st", [128, 8], F32)
    wtk = ["wt%d" % i for i in range(6)]

    for i in range(nb):
        Xi = X[i % 2]
        xk = "X%d" % (i % 2)
        k.dma("sp", Xi[:], src[128 * i:128 * (i + 1), :], [], [xk], xk)
        k.cp("pool", xbb[:], Xi[:], [xk], ["xbb"])
        for c in range(8):
            k.tr(pT[:, 0, c * 128:(c + 1) * 128], xbb[:, c * 128:(c + 1) * 128], identb[:], ["xbb", "identb"], ["pT0"], signal=(c == 7))
        k.cp("act", uT[:].rearrange("p c t -> p (c t)"), pT[:, 0, :], ["pT0"], ["uT"])
        for q in range(4):
            order = [(jj, c) for jj in (0, 1, 2, 3) for c in (2 * q, 2 * q + 1)]
            for idx, (jj, c) in enumerate(order):
                jl = 4 * (c - 2 * q) + jj
                rb = 64 * (jj // 2)
                for ri in range(2):
                    k.mm(pU[:, ri, jl * 128:(jl + 1) * 128], BTt[rb:rb + 64, jj % 2, c, ri, :], uT[rb:rb + 64, c, :], True, True,
                         ["BTt", "uT"], ["pU"], signal=(idx == 7 and ri == 1), rg=rb)
            Ur = pU[:, 0, :].rearrange("p (j t) -> p j t", j=8)
            Ui = pU[:, 1, :].rearrange("p (j t) -> p j t", j=8)
            CTq = CT[:, 8 * q:8 * q + 8, :]
            STq = ST[:, 8 * q:8 * q + 8, :]
            k.tt("dve", wt[0][:], Ur, CTq, ALU.mult, ["pU", "CT"], [wtk[0]])
            k.tt("dve", wt[1][:], Ui, STq, ALU.mult, ["pU", "ST"], [wtk[1]])
            k.tt("dve", wt[2][:], Ui, CTq, ALU.mult, ["pU", "CT"], [wtk[2]])
            k.tt("dve", wt[3][:], Ur, STq, ALU.mult, ["pU", "ST"], [wtk[3]])
            k.tt("pool", wt[4][:], wt[0][:], wt[1][:], ALU.add, [wtk[0], wtk[1]], [wtk[4]])
            k.tt("pool", wt[5][:], wt[2][:], wt[3][:], ALU.subtract, [wtk[2], wtk[3]], [wtk[5]])
            for jl in range(8):
                j = 8 * q + jl
                for ri in range(2):
                    S.op("dve", lambda e, jl=jl, j=j, ri=ri: e.tensor_tensor_scan(
                        out=gs[ri][:, jl, :], data0=sm[:, 4, j:j + 1].to_broadcast([128, 128]), data1=wt[4 + ri][:, jl, :],
                        initial=carry[:, j, ri:ri + 1], op0=ALU.mult, op1=ALU.add),
                        reads=[smk[4], wtk[4 + ri], "carry"], writes=["gs%d" % ri])
            jq = slice(8 * q, 8 * q + 8)
            k.tt("dve", cst[:, 0, :], gs[0][:, :, 127], c128[:, jq], ALU.mult, ["gs0", "c128"], ["cst0"])
            k.tt("dve", cst[:, 1, :], gs[1][:, :, 127], s128[:, jq], ALU.mult, ["gs1", "s128"], ["cst1"])
            k.tt("dve", cst[:, 2, :], gs[0][:, :, 127], s128[:, jq], ALU.mult, ["gs0", "s128"], ["cst2"])
            k.tt("dve", cst[:, 3, :], gs[1][:, :, 127], c128[:, jq], ALU.mult, ["gs1", "c128"], ["cst3"])
            k.tt("dve", carry[:, jq, 0], cst[:, 0, :], cst[:, 1, :], ALU.subtract, ["cst0", "cst1"], ["carry"])
            k.tt("dve", carry[:, jq, 1], cst[:, 2, :], cst[:, 3, :], ALU.add, ["cst2", "cst3"], ["carry"])
            k.tt("pool", wt[0][:], gs[0][:], CTq, ALU.mult, ["gs0", "CT"], [wtk[0]])
            k.tt("pool", wt[1][:], gs[1][:], STq, ALU.mult, ["gs1", "ST"], [wtk[1]])
            k.tt("pool", wt[2][:], gs[0][:], STq, ALU.mult, ["gs0", "ST"], [wtk[2]])
            k.tt("pool", wt[3][:], gs[1][:], CTq, ALU.mult, ["gs1", "CT"], [wtk[3]])
            k.tt("pool", hb[0][:], wt[0][:], wt[1][:], ALU.subtract, [wtk[0], wtk[1]], ["hb0"])
            k.tt("pool", hb[1][:], wt[2][:], wt[3][:], ALU.add, [wtk[2], wtk[3]], ["hb1"])
            for g2 in range(2):
                sl = slice(64 * g2, 64 * g2 + 64)
                for jl in range(8):
                    j = 8 * q + jl
                    g = 2 * j + g2
                    k.mm(pY[:, g * 16:(g + 1) * 16], hb[0][sl, jl, :], Cx[0][sl, j, :], True, False, ["hb0", "Cx0"], ["pY"], signal=False, rg=64 * g2)
                    k.mm(pY[:, g * 16:(g + 1) * 16], hb[1][sl, jl, :], Cx[1][sl, j, :], False, True, ["hb1", "Cx1"], ["pY"],
                         signal=(g2 == 1 and jl == 7), rg=64 * g2)
        k.tt("pool", Y2[:], Xi[:], bc[:, 0, :], ALU.mult, [xk, "bc"], ["Y2"])
        k.tt("dve", Y[:], pY[:], Y2[:], ALU.add, ["pY", "Y2"], ["Y"])
        k.tt("pool", Y2[:], Y[:], Y[:], ALU.mult, ["Y"], ["Y2"])
        k.ts("dve", Y2[:], Y2[:], 0.044715, 1.0, ALU.mult, ALU.add, ["Y2"], ["Y2"])
        k.tt("pool", Y2[:], Y2[:], Y[:], ALU.mult, ["Y2", "Y"], ["Y2"])
        k.act(Y2[:], Y2[:], AF.Tanh, ["Y2"], ["Y2"], scale=0.7978845608028654)
        k.ts("dve", Y2[:], Y2[:], 0.5, 0.5, ALU.mult, ALU.add, ["Y2"], ["Y2"])
        k.tt("pool", geb[:], Y2[:], Y[:], ALU.mult, ["Y2", "Y"], ["geb"])
        for c in range(8):
            k.tr(pT[:, 1, c * 128:(c + 1) * 128], geb[:, c * 128:(c + 1) * 128], identb[:], ["geb", "identb"], ["pT1"], signal=(c == 7))
        k.cp("act", yT[:].rearrange("p c t -> p (c t)"), pT[:, 1, :], ["pT1"], ["yT"])
        pUf = pU[:].rearrange("p a b -> p (a b)")
        for ch in range(4):
            for kk in range(8):
                k.mm(pUf[:, ch * 512:(ch + 1) * 512], yT[:, kk, :], wglu[:, kk, ch * 512:(ch + 1) * 512], kk == 0, kk == 7,
                     ["yT", "wglu"], ["pU"], signal=(kk == 7 and ch == 3))
        k.act(sgl[:], pU[:, 1, :], AF.Sigmoid, ["pU"], ["sgl"])
        k.tt("dve", sgl[:], pU[:, 0, :], sgl[:], ALU.mult, ["pU", "sgl"], ["sgl"])
        k.stt(Z[:], Xi[:], DN_ALPHA, sgl[:], ALU.mult, ALU.add, [xk, "sgl"], ["Z"])
        layer_norm_block(S, k, Z[:], "Z", O[:], "O", bc[:, 1, :], bc[:, 2, :], "bc", {"junk": junk, "st": st}, "s5")
        k.dma("sp", dst[128 * i:128 * (i + 1), :], O[:], ["O"], [], "s5out")
    k.dma("sp", h_out, carry[:], ["carry"], [], "s5hout")
    stage_end(S, es2)


def build(nb=16, stages=("rwkv", "mlp0", "s5", "mlp1")):
    nc = bass.Bass("TRN2", target_bir_lowering=False)
    NT = nb * 128
    d = {}

    def inp(name, shape):
        d[name] = nc.dram_tensor(name, list(shape), F32, kind="ExternalInput").ap()
        return d[name]

    xpad = inp("xpad", [NT + 1, C])
    ln_g = inp("ln_g", [4, C])
    ln_b = inp("ln_b", [4, C])
    mlp_w1 = inp("mlp_w1", [2, C, DFF])
    mlp_w2 = inp("mlp_w2", [2, DFF, C])
    rw = {}
    rw["mu"] = inp("rw_mu", [6, C])
    for nm, shp in (("w0", [1, C]), ("a0", [1, C]), ("k_k", [1, C]), ("k_a", [1, C]), ("r_k", [1, C]),
                    ("lnx_g", [1, C]), ("lnx_b", [1, C])):
        rw[nm] = inp("rw_" + nm, shp)
    for nm in ("w0", "a0", "k_k", "k_a", "r_k"):
        rw[nm] = rw[nm][0]
    rw["w1"] = inp("rw_w1", [C, 64]); rw["w2"] = inp("rw_w2", [64, C])
    rw["a1"] = inp("rw_a1", [C, 64]); rw["a2"] = inp("rw_a2", [64, C])
    rw["g1"] = inp("rw_g1", [C, 160]); rw["g2"] = inp("rw_g2", [160, C])
    for nm in ("wr", "wk", "wv", "wo"):
        rw[nm] = inp("rw_" + nm, [C, C])
    rw_s0 = inp("rw_s0", [128, 8, 64])
    s5 = {}
    s5["a_re"] = inp("s5_a_re", [64, 64]); s5["a_im"] = inp("s5_a_im", [64, 64])
    s5["log_dt"] = inp("s5_log_dt", [1, 64])
    s5["b_re"] = inp("s5_b_re", [64, 64, 16]); s5["b_im"] = inp("s5_b_im", [64, 64, 16])
    s5["c_re"] = inp("s5_c_re", [64, 16, 64]); s5["c_im"] = inp("s5_c_im", [64, 16, 64])
    s5["d"] = inp("s5_d", [1, C]); s5["w_glu"] = inp("s5_w_glu", [C, 2 * C])
    s5_h0 = inp("s5_h0", [128, 32, 2])
    rw_sout = nc.dram_tensor("rw_sout", [128, 8, 64], F32, kind="ExternalOutput").ap()
    s5_hout = nc.dram_tensor("s5_hout", [128, 32, 2], F32, kind="ExternalOutput").ap()
    out = nc.dram_tensor("out", [NT, C], F32, kind="ExternalOutput").ap()
    scr = [nc.dram_tensor("scr%d" % i, [NT, C], F32, kind="Internal").ap() for i in range(3)]
    with contextlib.ExitStack() as es:
        S = Sched(nc, es)
        k = K(S)
        consts = make_consts(S, k)
        cur = xpad[1:NT + 1, :]
        for si, stg in enumerate(stages):
            dst = out if si == len(stages) - 1 else scr[si]
            if stg == "rwkv":
                rwkv_stage(S, k, consts, nb, xpad, dst, rw, rw_s0, rw_sout, ln_g[0:1, :], ln_b[0:1, :])
            elif stg == "mlp0":
                mlp_stage(S, k, consts, nb, cur, dst, mlp_w1[0], mlp_w2[0], ln_g[1:2, :], ln_b[1:2, :], "m0")
            elif stg == "s5":
                s5_stage(S, k, consts, nb, cur, dst, s5, s5_h0, s5_hout, ln_g[2:3, :], ln_b[2:3, :])
            elif stg == "mlp1":
                mlp_stage(S, k, consts, nb, cur, dst, mlp_w1[1], mlp_w2[1], ln_g[3:4, :], ln_b[3:4, :], "m1")
            cur = dst
        S.finish()
    return nc


_NC_CACHE = {}


def _get_nc():
    if "nc" not in _NC_CACHE:
        _NC_CACHE["nc"] = build(nb=16)
    return _NC_CACHE["nc"]


def _param_map(inputs):
    f = lambda a: np.ascontiguousarray(np.asarray(a, dtype=np.float32))
    m = {
        "ln_g": f(inputs["ln_g"]), "ln_b": f(inputs["ln_b"]),
        "mlp_w1": f(inputs["mlp_w1"]), "mlp_w2": f(inputs["mlp_w2"]),
        "rw_mu": f(inputs["rw_mu"][0]),
        "rw_w0": f(inputs["rw_w0"]), "rw_a0": f(inputs["rw_a0"]), "rw_k_k": f(inputs["rw_k_k"]),
        "rw_k_a": f(inputs["rw_k_a"]), "rw_r_k": f(np.asarray(inputs["rw_r_k"]).reshape(1, C)),
        "rw_lnx_g": f(inputs["rw_lnx_g"]), "rw_lnx_b": f(inputs["rw_lnx_b"]),
        "rw_w1": f(inputs["rw_w1"][0]), "rw_w2": f(inputs["rw_w2"][0]),
        "rw_a1": f(inputs["rw_a1"][0]), "rw_a2": f(inputs["rw_a2"][0]),
        "rw_g1": f(inputs["rw_g1"][0]), "rw_g2": f(inputs["rw_g2"][0]),
        "rw_wr": f(inputs["rw_wr"][0]), "rw_wk": f(inputs["rw_wk"][0]),
        "rw_wv": f(inputs["rw_wv"][0]), "rw_wo": f(inputs["rw_wo"][0]),
        "s5_a_re": f(inputs["s5_a_re"][0]), "s5_a_im": f(inputs["s5_a_im"][0]),
        "s5_log_dt": f(inputs["s5_log_dt"]),
        "s5_b_re": f(inputs["s5_b_re"][0]), "s5_b_im": f(inputs["s5_b_im"][0]),
        "s5_c_re": f(inputs["s5_c_re"][0]), "s5_c_im": f(inputs["s5_c_im"][0]),
        "s5_d": f(inputs["s5_d"]), "s5_w_glu": f(inputs["s5_w_glu"][0]),
    }
    return m


def kernel(**inputs):
    x = np.asarray(inputs["x"], dtype=np.float32)
    B, T, _ = x.shape
    HALF = T // 2
    nc = _get_nc()
    pm = _param_map(inputs)
    xpads = []
    for core in range(8):
        b, h = core // 2, core % 2
        prev = np.zeros((1, C), np.float32) if h == 0 else x[b, HALF - 1:HALF]
        xpads.append(np.ascontiguousarray(np.concatenate([prev, x[b, h * HALF:(h + 1) * HALF]], 0)))
    zs = np.zeros((128, 8, 64), np.float32)
    zh = np.zeros((128, 32, 2), np.float32)
    in1 = [dict(pm, xpad=xpads[c], rw_s0=zs, s5_h0=zh) for c in range(8)]
    r1 = run_bass_kernel_spmd(nc, in1, core_ids=list(range(8))).results
    in2 = []
    for c in range(8):
        if c % 2 == 1:
            in2.append(dict(pm, xpad=xpads[c], rw_s0=np.ascontiguousarray(r1[c - 1]["rw_sout"]),
                            s5_h0=np.ascontiguousarray(r1[c - 1]["s5_hout"])))
        else:
            in2.append(dict(pm, xpad=xpads[c], rw_s0=zs, s5_h0=zh))
    r2 = run_bass_kernel_spmd(nc, in2, core_ids=list(range(8))).results
    out = np.empty((B, T, C), np.float32)
    for b in range(B):
        out[b, :HALF] = r1[2 * b]["out"]
        out[b, HALF:] = r2[2 * b + 1]["out"]
    return out
```
